# Optimizing a Trainium2 kernel written in Bass

```python
import math, functools
import jax, jax.numpy as jnp
from jax import lax
import numpy as np

D_MODEL = 1024
BATCH = 16
SEQ = 2048
DEPTH = 4

D_MIX = D_MODEL
HEAD_DIM = 64
ATTN_WIDTH = D_MIX // 2
CONV_WIDTH = D_MIX // 4
POOL_WIDTH = D_MIX - ATTN_WIDTH - CONV_WIDTH
N_ATTN_HEADS = ATTN_WIDTH // HEAD_DIM
POOL_WINDOWS = (2, 4, 8, 16)
N_POOL_GROUPS = len(POOL_WINDOWS)
POOL_GROUP = POOL_WIDTH // N_POOL_GROUPS
CONV_K = 3
MOBA_BLOCK = 256
MOBA_TOP_K = 3
NUM_BUCKETS = 32
MAX_DISTANCE = 1024
D_FF = ((8 * D_MODEL // 3 + 255) // 256) * 256
IN_COLS = 3 * ATTN_WIDTH + 3 * CONV_WIDTH + POOL_WIDTH
DEEPNORM_ALPHA = (2.0 * DEPTH) ** 0.25
DEEPNORM_BETA = (8.0 * DEPTH) ** -0.25
LN_EPS = 1e-5
NEG_INF = -1e30

kernel_name = 'hymba_style_moba_conv_pool_deepnorm'


def layer_norm(x, g, b):
    xf = x.astype(jnp.float32)
    mu = xf.mean(-1, keepdims=True)
    var = jnp.square(xf - mu).mean(-1, keepdims=True)
    return ((xf - mu) * lax.rsqrt(var + LN_EPS)).astype(x.dtype) * g + b


def causal_dwconv(x, w):
    S = x.shape[1]
    xp = jnp.pad(x, ((0, 0), (CONV_K - 1, 0), (0, 0)))
    return sum(xp[:, j:j + S] * w[j] for j in range(CONV_K))


def t5_bucket(dist):
    n = jnp.maximum(dist, 0)
    max_exact = NUM_BUCKETS // 2
    nf = jnp.maximum(n, 1).astype(jnp.float32)
    large = max_exact + (jnp.log(nf / max_exact) / math.log(MAX_DISTANCE / max_exact)
                         * (NUM_BUCKETS - max_exact)).astype(jnp.int32)
    large = jnp.minimum(large, NUM_BUCKETS - 1)
    return jnp.where(n < max_exact, n, large)


def _moba_query_block(i, n_sel, args):
    qh, selh, kh, vh, tab = args
    L = MOBA_BLOCK
    scale = HEAD_DIM ** -0.5
    r = jnp.arange(L)
    rel = r[:, None] - r[None, :]
    own_logits = (jnp.einsum('qd,ld->ql', qh, kh[i]).astype(jnp.float32) * scale
                  + tab[t5_bucket(rel)].astype(jnp.float32))
    own_logits = jnp.where(rel >= 0, own_logits, NEG_INF)
    if n_sel == 0:
        p = jax.nn.softmax(own_logits, axis=-1).astype(vh.dtype)
        return jnp.einsum('ql,ld->qd', p, vh[i])
    k_sel = kh[selh]
    v_sel = vh[selh]
    dist = (i - selh)[:, :, None] * L + rel[:, None, :]
    sel_logits = (jnp.einsum('qd,qnld->qnl', qh, k_sel).astype(jnp.float32) * scale
                  + tab[t5_bucket(dist)].astype(jnp.float32))
    logits = jnp.concatenate([sel_logits.reshape(L, n_sel * L), own_logits], axis=-1)
    p = jax.nn.softmax(logits, axis=-1).astype(vh.dtype)
    p_sel = p[:, :n_sel * L].reshape(L, n_sel, L)
    p_own = p[:, n_sel * L:]
    return jnp.einsum('qnl,qnld->qd', p_sel, v_sel) + jnp.einsum('ql,ld->qd', p_own, vh[i])


def moba_attention(q, k, v, rel_bias):
    B, H, S, Dh = q.shape
    L = MOBA_BLOCK
    nb = -(-S // L)
    pad = nb * L - S
    if pad:
        cfg = ((0, 0), (0, 0), (0, pad), (0, 0))
        q, k, v = jnp.pad(q, cfg), jnp.pad(k, cfg), jnp.pad(v, cfg)
    qb = q.reshape(B, H, nb, L, Dh)
    kb = k.reshape(B, H, nb, L, Dh)
    vb = v.reshape(B, H, nb, L, Dh)
    k_mean = kb.astype(jnp.float32).mean(axis=3).astype(k.dtype)
    k_flat = kb.reshape(B * H, nb, L, Dh)
    v_flat = vb.reshape(B * H, nb, L, Dh)
    tab = jnp.broadcast_to(rel_bias[None], (B,) + rel_bias.shape).reshape(B * H, NUM_BUCKETS)
    outs = []
    for i in range(nb):
        q_i = qb[:, :, i]
        n_sel = min(MOBA_TOP_K, i)
        if n_sel:
            gate = jnp.einsum('bhqd,bhnd->bhqn', q_i, k_mean[:, :, :i]).astype(jnp.float32)
            _, sel = lax.top_k(gate, n_sel)
            sel = sel.reshape(B * H, L, n_sel)
        else:
            sel = jnp.zeros((B * H, L, 0), jnp.int32)
        step = functools.partial(_moba_query_block, i, n_sel)
        outs.append(lax.map(step, (q_i.reshape(B * H, L, Dh), sel, k_flat, v_flat, tab)))
    out = jnp.concatenate(outs, axis=1).reshape(B, H, nb * L, Dh)
    return out[:, :, :S]


def pool_mixer(p, w_pool, pool_scale):
    B, S, _ = p.shape
    pg = p.reshape(B, S, N_POOL_GROUPS, POOL_GROUP)
    pf = pg.astype(jnp.float32)
    c = jnp.cumsum(pf, axis=1)
    pos = jnp.arange(1, S + 1, dtype=jnp.float32)
    means = []
    for g, w in enumerate(POOL_WINDOWS):
        cg = c[:, :, g]
        lag = jnp.pad(cg, ((0, 0), (w, 0), (0, 0)))[:, :S]
        means.append((cg - lag) / jnp.minimum(pos, float(w))[None, :, None])
    pooled = (jnp.stack(means, axis=2) - pf).astype(p.dtype)
    y = jnp.einsum('bsgc,gcd->bsgd', pooled, w_pool)
    return y.reshape(B, S, POOL_WIDTH) * pool_scale


def to_heads(t):
    B, S, _ = t.shape
    return t.reshape(B, S, -1, HEAD_DIM).transpose(0, 2, 1, 3)


def hybrid_layer(x, w_in, conv_w, w_pool, pool_scale, w_out, ln1_g, ln1_b,
                 w_up, ffn_conv_w, ffn_conv_b, w_down, ln2_g, ln2_b, rel_bias):
    B, S, _ = x.shape
    h = x @ w_in
    sizes = [ATTN_WIDTH, ATTN_WIDTH, ATTN_WIDTH, CONV_WIDTH, CONV_WIDTH, CONV_WIDTH]
    q, k, v, cb, cc, cx, pin = jnp.split(h, list(np.cumsum(sizes)), axis=-1)
    attn = moba_attention(to_heads(q), to_heads(k), to_heads(v), rel_bias)
    attn = attn.transpose(0, 2, 1, 3).reshape(B, S, ATTN_WIDTH)
    conv = cb * causal_dwconv(cc * cx, conv_w)
    pool = pool_mixer(pin, w_pool, pool_scale)
    mix = jnp.concatenate([attn, conv, pool], axis=-1) @ w_out
    x = layer_norm(DEEPNORM_ALPHA * x + mix, ln1_g, ln1_b)
    up = causal_dwconv(x @ w_up, ffn_conv_w) + ffn_conv_b
    u, g = jnp.split(up, 2, axis=-1)
    ff = (u * jax.nn.silu(g)) @ w_down
    return layer_norm(DEEPNORM_ALPHA * x + ff, ln2_g, ln2_b)


def setup_inputs(seed: int = 0) -> dict:
    key = jax.random.key(seed)
    ks = jax.random.split(key, 16)
    nrm = jax.random.normal
    return {
        'x': nrm(ks[0], (BATCH, SEQ, D_MODEL), jnp.float32),
        'w_in': nrm(ks[1], (DEPTH, D_MODEL, IN_COLS), jnp.float32) * D_MODEL ** -0.5,
        'conv_w': nrm(ks[2], (DEPTH, CONV_K, CONV_WIDTH), jnp.float32) * CONV_K ** -0.5,
        'w_pool': nrm(ks[3], (DEPTH, N_POOL_GROUPS, POOL_GROUP, POOL_GROUP), jnp.float32) * POOL_GROUP ** -0.5,
        'pool_scale': 1.0 + 0.02 * nrm(ks[4], (DEPTH, POOL_WIDTH), jnp.float32),
        'w_out': nrm(ks[5], (DEPTH, D_MIX, D_MODEL), jnp.float32) * (D_MIX ** -0.5 * DEEPNORM_BETA),
        'ln1_g': 1.0 + 0.02 * nrm(ks[6], (DEPTH, D_MODEL), jnp.float32),
        'ln1_b': 0.02 * nrm(ks[7], (DEPTH, D_MODEL), jnp.float32),
        'w_up': nrm(ks[8], (DEPTH, D_MODEL, 2 * D_FF), jnp.float32) * D_MODEL ** -0.5,
        'ffn_conv_w': nrm(ks[9], (DEPTH, CONV_K, 2 * D_FF), jnp.float32) * CONV_K ** -0.5,
        'ffn_conv_b': 0.02 * nrm(ks[10], (DEPTH, 2 * D_FF), jnp.float32),
        'w_down': nrm(ks[11], (DEPTH, D_FF, D_MODEL), jnp.float32) * (D_FF ** -0.5 * DEEPNORM_BETA),
        'ln2_g': 1.0 + 0.02 * nrm(ks[12], (DEPTH, D_MODEL), jnp.float32),
        'ln2_b': 0.02 * nrm(ks[13], (DEPTH, D_MODEL), jnp.float32),
        'rel_bias': 0.5 * nrm(ks[14], (N_ATTN_HEADS, NUM_BUCKETS), jnp.float32),
    }


def reference(x, w_in, conv_w, w_pool, pool_scale, w_out, ln1_g, ln1_b,
              w_up, ffn_conv_w, ffn_conv_b, w_down, ln2_g, ln2_b, rel_bias):
    for l in range(DEPTH):
        x = hybrid_layer(x, w_in[l], conv_w[l], w_pool[l], pool_scale[l], w_out[l],
                         ln1_g[l], ln1_b[l], w_up[l], ffn_conv_w[l], ffn_conv_b[l],
                         w_down[l], ln2_g[l], ln2_b[l], rel_bias)
    return x
```

```python
import math
import numpy as np
import ml_dtypes
import jax
import jax.numpy as jnp
import concourse.bass as bass
import concourse.mybir as mybir
from concourse.bass_utils import run_bass_kernel_spmd

F32 = mybir.dt.float32
BF16 = mybir.dt.bfloat16
ALU = mybir.AluOpType
AF = mybir.ActivationFunctionType
AX = mybir.AxisListType

D = 1024
S = 2048
DEPTH = 4
NSEQ = 2
DFF = 2816
NJ = DFF // 128
INC = 2560
ALPHA = (2.0 * DEPTH) ** 0.25
EPS = 1e-5
SCALE = 64 ** -0.5
EBW = 1408
BIG = 30000.0

V_LN = 0
V_CVW = V_LN + 128
V_PSC = V_CVW + 24
V_FCW = V_PSC + 8
V_FCB = V_FCW + 528
V_TB = V_FCB + 176
V_RDV = V_TB + 8
V_RW = V_RDV + 32
NV = V_RW + 2


class Prog:
    def __init__(self):
        self.ops = []
        self.last_w = {}
        self.readers = {}
        self.cnt = {"pe": 0, "act": 0, "dve": 0, "pool": 0}
        self.dma_cnt = {}
        self.barrier = None

    def op(self, eng, fn, reads=(), writes=(), dkey=None, ndma=1, q=None):
        i = len(self.ops)
        deps = set()
        for r in reads:
            if r in self.last_w:
                deps.add(self.last_w[r])
        for w in writes:
            if w in self.last_w:
                deps.add(self.last_w[w])
            for rd in self.readers.get(w, ()):
                deps.add(rd)
        for r in reads:
            self.readers.setdefault(r, []).append(i)
        for w in writes:
            self.last_w[w] = i
            self.readers[w] = []
        deps.discard(i)
        if eng == "dma":
            self.dma_cnt[dkey] = self.dma_cnt.get(dkey, 0) + 16 * ndma
            done = (dkey, self.dma_cnt[dkey])
            queue = q or "sp"
        else:
            self.cnt[eng] += 1
            done = (eng, self.cnt[eng])
            queue = eng
        self.ops.append(dict(queue=queue, eng=eng, fn=fn, deps=deps, done=done,
                             dkey=dkey, barrier=self.barrier))
        return i

    def add_barrier(self):
        snap = {}
        for o in self.ops:
            k, v = o["done"]
            snap[k] = max(snap.get(k, 0), v)
        self.barrier = snap


def build_program(nc, nseq=NSEQ, nlayers=DEPTH, do_mixer=True, do_ffn=True, do_prepass=True, stop=None, stop_t=0, stop_L=0):
    P = Prog()
    dt = lambda n, s, d, k: nc.dram_tensor(n, s, d, kind=k).ap()
    x_d = dt("x", [NSEQ, S, D], F32, "ExternalInput")
    win_d = dt("w_in", [DEPTH, D, INC], F32, "ExternalInput")
    wout_d = dt("w_out", [DEPTH, D, D], F32, "ExternalInput")
    wup_d = dt("w_up", [DEPTH, D, 2 * DFF], F32, "ExternalInput")
    wdn_d = dt("w_down", [DEPTH, DFF, D], F32, "ExternalInput")
    wpl_d = dt("wpool_bd", [DEPTH, 128, 2, 128], F32, "ExternalInput")
    vec_d = dt("vecs", [128, NV], F32, "ExternalInput")
    bst_d = dt("bstrip", [8, 128, EBW], F32, "ExternalInput")
    idn_d = dt("ident", [128, 128], F32, "ExternalInput")
    cm_d = dt("cm", [128, 512], F32, "ExternalInput")
    eal_d = dt("eall", [8, 1024], BF16, "ExternalInput")
    out_d = dt("out", [NSEQ, S, D], F32, "ExternalOutput")
    wbin = dt("wbin", [DEPTH, 20, 128, 8 * 128], BF16, "Internal")
    wbout = dt("wbout", [DEPTH, 8, 128, 8 * 128], BF16, "Internal")
    wbup = dt("wbup", [DEPTH, 44, 128, 8 * 128], BF16, "Internal")
    wbdn = dt("wbdn", [DEPTH, 8, 128, NJ * 128], BF16, "Internal")
    wbpl = dt("wbpl", [DEPTH, 128, 256], BF16, "Internal")

    from contextlib import ExitStack
    es = ExitStack()
    sb = lambda n, s, d: es.enter_context(nc.sbuf_tensor(n, s, d))
    xres = sb("xres", [128, 8, S], F32)
    EB = sb("EB", [128, 8, EBW], BF16)
    vecs = sb("vecs_sb", [128, NV], F32)
    ident = sb("ident_sb", [128, 128], F32)
    cm = sb("cm_sb", [128, 2, 8, 4, 8], F32)
    eall = sb("eall_sb", [8, 8, 128], BF16)
    ones = sb("ones_sb", [128, 128], BF16)
    kmT = sb("kmT", [128, 4, 8], BF16)
    WORKF = 28160
    work = sb("work", [128, WORKF], F32)
    ps = es.enter_context(nc.psum_tensor("ps", [128, 4096], F32))

    def bank(b):
        return ps[:, b * 512:(b + 1) * 512]

    class Carver:
        def __init__(self):
            self.off = 0

        def f32(self, ncol):
            a = self.off
            self.off += ncol
            assert self.off <= WORKF, self.off
            return work[:, a:a + ncol]

        def bf(self, ncol):
            n32 = (ncol + 1) // 2
            a = self.off
            self.off += n32
            assert self.off <= WORKF, self.off
            return work[:, a:a + n32].bitcast(BF16)

    SL = 528
    cv = Carver()
    kT = cv.bf(4 * S).rearrange("p (c t) -> p c t", c=4)
    Vaug = cv.bf(16 * 4 * 192).rearrange("p (l j c) -> p l j c", l=16, j=4)
    xbf = cv.bf(8 * 512).rearrange("p (c t) -> p c t", c=8)
    qT = cv.bf(4 * 512).rearrange("p (c t) -> p c t", c=4)
    mixT = cv.bf(8 * 512).rearrange("p (c t) -> p c t", c=8)
    wbufM = [cv.bf(1024).rearrange("p (k n) -> p k n", k=8) for _ in range(4)]
    wpb = cv.bf(256).rearrange("p (c n) -> p c n", c=2)
    slotsM = [cv.f32(SL) for _ in range(8)]
    Tcv = [cv.f32(514) for _ in range(2)]
    pinb = [cv.f32(528) for _ in range(2)]
    plb = cv.bf(512)
    Pf = [cv.f32(512) for _ in range(2)]
    Pb = [cv.bf(512) for _ in range(3)]
    rs = cv.f32(512)
    gm = cv.f32(256).rearrange("p (h s n) -> p h s n", h=8, s=4)
    top = cv.f32(256).rearrange("p (h s n) -> p h s n", h=8, s=4)
    Mm = cv.f32(256).rearrange("p (h s n) -> p h s n", h=8, s=4)
    MT = [cv.bf(512) for _ in range(2)]
    kms = cv.f32(2)
    mixer_end = cv.off
    cf = Carver()
    x1bf = cf.bf(8 * 1026).rearrange("p (c t) -> p c t", c=8)
    hid = cf.bf(NJ * 1024).rearrange("p (j t) -> p j t", j=NJ)
    wbufU = [cf.bf(1024).rearrange("p (k n) -> p k n", k=8) for _ in range(4)]
    wbufD = [cf.bf(NJ * 128).rearrange("p (j n) -> p j n", j=NJ) for _ in range(2)]
    slotsF = [cf.f32(SL) for _ in range(14)]
    xhalo = cf.bf(16).rearrange("p (c t) -> p c t", c=8)
    ci = Carver()
    xin = [ci.f32(1024) for _ in range(2)]
    bstg = [ci.f32(EBW) for _ in range(2)]

    def vcol(off, n=1):
        return vecs[:, off:off + n]

    P.op("dma", lambda e: e.dma_start(out=vecs[:], in_=vec_d), writes=["vecs"], dkey="c_vecs")
    P.op("dma", lambda e: e.dma_start(out=ident[:], in_=idn_d), writes=["ident"], dkey="c_ident")
    P.op("dma", lambda e: e.dma_start(out=cm[:].rearrange("p a h s n -> p (a h s n)"), in_=cm_d),
         writes=["cm"], dkey="c_cm")
    P.op("dma", lambda e: e.dma_start(out=eall[:].rearrange("p a n -> p (a n)"), in_=eal_d),
         writes=["eall"], dkey="c_eall")
    P.op("pool", lambda e: e.memset(ones[:], 1.0), writes=["ones"])
    P.op("pool", lambda e: e.memset(kmT[:].rearrange("p a n -> p (a n)"), 0.0), writes=["kmT"])

    def prepass(L):
        jobs = []
        src = win_d[L].rearrange("(kc p) n -> p kc n", p=128)
        for c in range(20):
            jobs.append(("wbin%d" % L, wbin[L, c].rearrange("p (k n) -> p k n", k=8), src[:, :, c * 128:(c + 1) * 128]))
        jobs.append(("wbpl%d" % L, wbpl[L], wpl_d[L].rearrange("p c n -> p (c n)")))
        src = wout_d[L].rearrange("(kc p) n -> p kc n", p=128)
        for c in range(8):
            jobs.append(("wbout%d" % L, wbout[L, c].rearrange("p (k n) -> p k n", k=8), src[:, :, c * 128:(c + 1) * 128]))
        src = wup_d[L].rearrange("(kc p) n -> p kc n", p=128)
        for c in range(44):
            jobs.append(("wbup%d" % L, wbup[L, c].rearrange("p (k n) -> p k n", k=8), src[:, :, c * 128:(c + 1) * 128]))
        src = wdn_d[L].rearrange("(j p) n -> p j n", p=128)
        for c in range(8):
            jobs.append(("wbdn%d" % L, wbdn[L, c].rearrange("p (j n) -> p j n", j=NJ), src[:, :, c * 128:(c + 1) * 128]))
        PB = 1000
        for i in range(0, len(jobs), PB):
            grp = jobs[i:i + PB]

            def f(e, grp=grp, L=L):
                r = None
                for gi, (_, o_, i_) in enumerate(grp):
                    r = e.dma_start(out=o_, in_=i_)
                    if gi < len(grp) - 1:
                        r.then_inc(SEM["pp%d" % L], 16)
                return r
            P.op("dma", f, writes=sorted({g[0] for g in grp}), dkey="pp%d" % L, ndma=len(grp), q="pool")

    SEM = {}
    for L in range(nlayers if do_prepass else 0):
        prepass(L)

    for h in range(8):
        b = h % 2
        P.op("dma", lambda e, h=h, b=b: e.dma_start(out=bstg[b], in_=bst_d[h]),
             writes=["bstg%d" % b], dkey="bstg%d" % b)
        P.op("act", lambda e, h=h, b=b: e.activation(out=EB[:, h, :], in_=bstg[b], func=AF.Exp),
             reads=["bstg%d" % b], writes=["EB"])

    rot = {"i": 0}

    def next_bank():
        b = rot["i"] % 4
        rot["i"] += 1
        return b

    wrot = {"i": 0}

    def layer_norm(L, k, t0, ncols, xnames, slots, sname):
        mean, var, rstd, nmr, t1, t2 = [s[:, 0:ncols] for s in slots[0:6]]
        n = [sname + str(i) for i in range(6)]
        gof = V_LN + L * 32 + k * 8
        bof = V_LN + L * 32 + (k + 1) * 8
        P.op("act", lambda e: e.activation(out=mean, in_=bank(6)[:, 0:ncols], func=AF.Identity, scale=1.0 / D),
             reads=["ps6"], writes=[n[0]])
        P.op("pool", lambda e: e.tensor_tensor(out=var, in0=mean, in1=mean, op=ALU.mult),
             reads=[n[0]], writes=[n[1]])
        P.op("dve", lambda e: e.scalar_tensor_tensor(out=var, in0=bank(7)[:, 0:ncols], scalar=1.0 / D, in1=var,
                                                      op0=ALU.mult, op1=ALU.subtract),
             reads=["ps7", n[1]], writes=[n[1]])
        P.op("act", lambda e: e.activation(out=var, in_=var, func=AF.Sqrt, bias=EPS, scale=1.0),
             reads=[n[1]], writes=[n[1]])
        P.op("dve", lambda e: e.reciprocal(out=rstd, in_=var),
             reads=[n[1]], writes=[n[2]])
        P.op("pool", lambda e: e.tensor_scalar(out=nmr, in0=mean, scalar1=-1.0, scalar2=None, op0=ALU.mult),
             reads=[n[0]], writes=[n[3]])
        P.op("pool", lambda e: e.tensor_tensor(out=nmr, in0=nmr, in1=rstd, op=ALU.mult),
             reads=[n[3], n[2]], writes=[n[3]])
        for dc in range(8):
            xr = xres[:, dc, t0:t0 + ncols]
            P.op("dve", lambda e, xr=xr: e.tensor_tensor(out=t1, in0=xr, in1=rstd, op=ALU.mult),
                 reads=[xnames[dc], n[2]], writes=[n[4]])
            P.op("pool", lambda e: e.tensor_tensor(out=t2, in0=t1, in1=nmr, op=ALU.add),
                 reads=[n[3], n[4]], writes=[n[5]])
            P.op("act", lambda e, xr=xr, dc=dc: e.activation(out=xr, in_=t2, func=AF.Identity,
                                                             bias=vcol(bof + dc), scale=vcol(gof + dc)),
                 reads=[n[5], "vecs"], writes=[xnames[dc]])

    def proj_ln(L, k, t0, xnames, nk, lhs_fn, rhs_fn, wload_fn, wnames, slots, sname, rhs_names):
        zb = [slots[6][:, 0:256].bitcast(BF16), slots[6][:, 256:512].bitcast(BF16)]
        zq = [slots[7][:, 0:256].bitcast(BF16), slots[7][:, 256:512].bitcast(BF16)]
        for dc in range(8):
            wi = wload_fn(dc)
            b = next_bank()
            bn = "ps%d" % b

            def mm(e, dc=dc, wi=wi, b=b):
                r = None
                for c in range(nk):
                    r = e.matmul(bank(b), lhsT=lhs_fn(wi, c), rhs=rhs_fn(c), start=(c == 0), stop=(c == nk - 1))
                return r
            P.op("pe", mm, reads=[wnames[wi]] + rhs_names, writes=[bn])
            xr = xres[:, dc, t0:t0 + 512]
            P.op("dve", lambda e, xr=xr, b=b: e.scalar_tensor_tensor(
                out=xr, in0=xr, scalar=ALPHA, in1=bank(b), op0=ALU.mult, op1=ALU.add),
                reads=[bn, xnames[dc]], writes=[xnames[dc]])
            r2 = dc % 2
            P.op("pool", lambda e, xr=xr, r2=r2: e.tensor_copy(out=zb[r2], in_=xr),
                 reads=[xnames[dc]], writes=[sname + "zb%d" % r2])
            P.op("pool", lambda e, xr=xr, r2=r2: e.tensor_tensor(out=zq[r2], in0=xr, in1=xr, op=ALU.mult),
                 reads=[xnames[dc]], writes=[sname + "zq%d" % r2])

            def st(e, dc=dc, r2=r2):
                e.matmul(bank(6), lhsT=ones[:], rhs=zb[r2], start=(dc == 0), stop=(dc == 7))
                return e.matmul(bank(7), lhsT=ones[:], rhs=zq[r2], start=(dc == 0), stop=(dc == 7))
            P.op("pe", st, reads=[sname + "zb%d" % r2, sname + "zq%d" % r2, "ones"], writes=["ps6", "ps7"])
        layer_norm(L, k, t0, 512, xnames, slots, sname)

    def mixer(L, stop=stop):
        if L != stop_L:
            stop = None
        P.add_barrier()
        Vv = Vaug.rearrange("p l j (a d) -> p l j a d", a=3)
        P.op("pool", lambda e: e.memset(Vaug.rearrange("p l j c -> p (l j c)"), 1.0), writes=["Vall"])
        P.op("pool", lambda e: e.memset(Tcv[0][:, 0:2], 0.0), writes=["T0"])
        P.op("pool", lambda e: e.memset(Tcv[1][:, 0:2], 0.0), writes=["T1"])
        P.op("pool", lambda e: e.memset(pinb[0][:, 0:16], 0.0), writes=["pin0"])
        P.op("pool", lambda e: e.memset(pinb[1][:, 0:16], 0.0), writes=["pin1"])
        P.op("dma", lambda e: e.dma_start(out=wpb.rearrange("p c n -> p (c n)"), in_=wbpl[L]),
             reads=["wbpl%d" % L], writes=["wpb"], dkey="wpb")

        def wload(src_ap, srcname):
            wi = wrot["i"] % 4
            wrot["i"] += 1
            P.op("dma", lambda e, wi=wi: e.dma_start(out=wbufM[wi].rearrange("p k n -> p (k n)"), in_=src_ap),
                 reads=[srcname], writes=["wM%d" % wi], dkey="wM%d" % wi)
            return wi

        if stop == 0:
            return
        for t in range(4):
            t0 = t * 512
            xn = ["xres.%d.%d" % (t, dc) for dc in range(8)]
            P.op("pool", lambda e, t0=t0: e.tensor_copy(out=xbf, in_=xres[:, :, t0:t0 + 512]),
                 reads=xn, writes=["xbf"])

            def inproj(chunk):
                wi = wload(wbin[L, chunk], "wbin%d" % L)
                b = next_bank()

                def mm(e, wi=wi, b=b):
                    r = None
                    for kc in range(8):
                        r = e.matmul(bank(b), lhsT=wbufM[wi][:, kc, :], rhs=xbf[:, kc, :],
                                     start=(kc == 0), stop=(kc == 7))
                    return r
                P.op("pe", mm, reads=["wM%d" % wi, "xbf"], writes=["ps%d" % b])
                return b

            if stop == 10 and t == stop_t:
                return
            for j in range(4):
                b = inproj(j)
                if stop == 11:
                    return
                P.op("act", lambda e, j=j, b=b: e.activation(out=qT[:, j, :], in_=bank(b), func=AF.Copy),
                     reads=["ps%d" % b], writes=["qT%d" % j])
            if stop == 12 and t == stop_t:
                return
            for j in range(4):
                b = inproj(4 + j)
                P.op("act", lambda e, j=j, b=b, t0=t0: e.activation(out=kT[:, j, t0:t0 + 512], in_=bank(b), func=AF.Copy),
                     reads=["ps%d" % b], writes=["kT.%d.%d" % (j, t)])
                if stop == 13:
                    return
                P.op("dve", lambda e, j=j, t0=t0: e.tensor_reduce(out=kms, in_=kT[:, j, t0:t0 + 512].rearrange("p (a c) -> p a c", a=2),
                                                           axis=AX.X, op=ALU.add),
                     reads=["kT.%d.%d" % (j, t)], writes=["kms"])
                if stop == 14:
                    return
                P.op("dve", lambda e, j=j, t=t: e.tensor_scalar(out=kmT[:, j, 2 * t:2 * t + 2], in0=kms,
                                                                 scalar1=1.0 / 256, scalar2=None, op0=ALU.mult),
                     reads=["kms"], writes=["kmT"])
            if stop == 1 and t == stop_t:
                return
            for vj in range(4):
                wi = wload(wbin[L, 8 + vj], "wbin%d" % L)
                b = next_bank()

                def mmv(e, wi=wi, b=b):
                    r = None
                    for s4 in range(4):
                        for kc in range(8):
                            r = e.matmul(bank(b)[:, s4 * 128:(s4 + 1) * 128], lhsT=xbf[:, kc, s4 * 128:(s4 + 1) * 128],
                                         rhs=wbufM[wi][:, kc, :], start=(kc == 0), stop=(kc == 7))
                    return r
                P.op("pe", mmv, reads=["wM%d" % wi, "xbf"], writes=["ps%d" % b])
                bv = bank(b).rearrange("p (s a d) -> p s a d", s=4, a=2)
                P.op("act", lambda e, vj=vj, bv=bv, t=t: e.activation(out=Vv[:, 4 * t:4 * t + 4, vj, 0, :], in_=bv[:, :, 0, :], func=AF.Copy),
                     reads=["ps%d" % b, "Vall"], writes=["V.%d.%d.a" % (t, vj)])
                P.op("dve", lambda e, vj=vj, bv=bv, t=t: e.tensor_copy(out=Vv[:, 4 * t:4 * t + 4, vj, 2, :], in_=bv[:, :, 1, :]),
                     reads=["ps%d" % b, "Vall", "V.%d.%d.a" % (t, vj)], writes=["V.%d.%d.b" % (t, vj)])
            if stop == 2 and t == stop_t:
                return
            for c in range(2):
                cxs, acc_ = slotsM[0][:, 0:512], slotsM[1][:, 0:512]
                Tn = "T%d" % c
                wof = V_CVW + L * 6 + c * 3
                b = inproj(16 + c)
                P.op("act", lambda e, b=b: e.activation(out=cxs, in_=bank(b), func=AF.Copy),
                     reads=["ps%d" % b], writes=["m_s0"])
                b = inproj(14 + c)
                P.op("dve", lambda e, b=b, c=c: e.tensor_tensor(out=Tcv[c][:, 2:514], in0=bank(b), in1=cxs, op=ALU.mult),
                     reads=["ps%d" % b, "m_s0"], writes=[Tn])
                P.op("act", lambda e, c=c, wof=wof: e.activation(out=acc_, in_=Tcv[c][:, 2:514], func=AF.Identity, scale=vcol(wof + 2)),
                     reads=[Tn, "vecs"], writes=["m_s1"])
                P.op("dve", lambda e, c=c, wof=wof: e.scalar_tensor_tensor(out=acc_, in0=Tcv[c][:, 1:513], scalar=vcol(wof + 1),
                                                                            in1=acc_, op0=ALU.mult, op1=ALU.add),
                     reads=[Tn, "m_s1"], writes=["m_s1"])
                P.op("dve", lambda e, c=c, wof=wof: e.scalar_tensor_tensor(out=acc_, in0=Tcv[c][:, 0:512], scalar=vcol(wof),
                                                                            in1=acc_, op0=ALU.mult, op1=ALU.add),
                     reads=[Tn, "m_s1"], writes=["m_s1"])
                b = inproj(12 + c)
                P.op("dve", lambda e, b=b, c=c: e.tensor_tensor(out=mixT[:, 4 + c, :], in0=acc_, in1=bank(b), op=ALU.mult),
                     reads=["ps%d" % b, "m_s1"], writes=["mix%d" % (4 + c)])
                P.op("pool", lambda e, c=c: e.tensor_copy(out=Tcv[c][:, 0:2], in_=Tcv[c][:, 512:514]),
                     reads=[Tn], writes=[Tn])
            if stop == 3 and t == stop_t:
                return
            for c in range(2):
                pn = "pin%d" % c
                pb_ = pinb[c]
                s2, s4_, s8, s16 = slotsM[2], slotsM[3], slotsM[4], slotsM[5]
                b = inproj(18 + c)
                P.op("act", lambda e, b=b, pb_=pb_: e.activation(out=pb_[:, 16:528], in_=bank(b), func=AF.Copy),
                     reads=["ps%d" % b], writes=[pn])
                P.op("pool", lambda e, pb_=pb_: e.tensor_tensor(out=s2[:, 1:528], in0=pb_[:, 1:528], in1=pb_[:, 0:527], op=ALU.add),
                     reads=[pn], writes=["m_s2"])
                if c == 0:
                    P.op("pool", lambda e: e.tensor_tensor(out=s4_[64:128, 3:528], in0=s2[64:128, 3:528], in1=s2[64:128, 1:526], op=ALU.add),
                         reads=["m_s2"], writes=["m_s3"])
                    lo, hi, lon, hin = s2, s4_, "m_s2", "m_s3"
                else:
                    P.op("pool", lambda e: e.tensor_tensor(out=s4_[:, 3:528], in0=s2[:, 3:528], in1=s2[:, 1:526], op=ALU.add),
                         reads=["m_s2"], writes=["m_s3"])
                    P.op("pool", lambda e: e.tensor_tensor(out=s8[:, 7:528], in0=s4_[:, 7:528], in1=s4_[:, 3:524], op=ALU.add),
                         reads=["m_s3"], writes=["m_s4"])
                    P.op("pool", lambda e: e.tensor_tensor(out=s16[64:128, 15:528], in0=s8[64:128, 15:528], in1=s8[64:128, 7:520], op=ALU.add),
                         reads=["m_s4"], writes=["m_s5"])
                    lo, hi, lon, hin = s8, s16, "m_s4", "m_s5"
                for (src, sn, p0) in ((lo, lon, 0), (hi, hin, 64)):
                    tmp16 = slotsM[1][:, 512:528]
                    if t == 0:
                        P.op("pool", lambda e, src=src, p0=p0, c=c: e.tensor_tensor(
                            out=tmp16[p0:p0 + 64, :], in0=src[p0:p0 + 64, 16:32],
                            in1=vecs[p0:p0 + 64, V_RDV + c * 16:V_RDV + (c + 1) * 16], op=ALU.mult),
                            reads=[sn, "vecs"], writes=["m_tmp16"])
                    P.op("pool", lambda e, src=src, p0=p0, c=c: e.tensor_scalar(
                        out=src[p0:p0 + 64, 16:528], in0=src[p0:p0 + 64, 16:528],
                        scalar1=vecs[p0:p0 + 64, V_RW + c:V_RW + c + 1], scalar2=None, op0=ALU.mult),
                        reads=[sn, "vecs"], writes=[sn])
                    P.op("pool", lambda e, src=src, p0=p0, pb_=pb_: e.tensor_tensor(
                        out=plb[p0:p0 + 64, :], in0=src[p0:p0 + 64, 16:528], in1=pb_[p0:p0 + 64, 16:528], op=ALU.subtract),
                        reads=[sn, pn], writes=["plb"])
                    if t == 0:
                        P.op("pool", lambda e, p0=p0, pb_=pb_: e.tensor_tensor(
                            out=plb[p0:p0 + 64, 0:16], in0=tmp16[p0:p0 + 64, :], in1=pb_[p0:p0 + 64, 16:32], op=ALU.subtract),
                            reads=["m_tmp16", pn, "plb"], writes=["plb"])
                P.op("pool", lambda e, pb_=pb_: e.tensor_copy(out=pb_[:, 0:16], in_=pb_[:, 512:528]),
                     reads=[pn], writes=[pn])
                b = next_bank()
                P.op("pe", lambda e, b=b, c=c: e.matmul(bank(b), lhsT=wpb[:, c, :], rhs=plb, start=True, stop=True),
                     reads=["wpb", "plb"], writes=["ps%d" % b])
                P.op("act", lambda e, b=b, c=c: e.activation(out=mixT[:, 6 + c, :], in_=bank(b), func=AF.Identity,
                                                             scale=vcol(V_PSC + L * 2 + c)),
                     reads=["ps%d" % b, "vecs"], writes=["mix%d" % (6 + c)])

            if stop == 4 and t == stop_t:
                return
            masked_tile = t >= 2
            if masked_tile:
                def gate(e):
                    r = None
                    for h in range(8):
                        j, r0 = h // 2, 64 * (h % 2)
                        for s4 in range(4):
                            o = (h * 4 + s4) * 8
                            r = e.matmul(bank(5)[:, o:o + 8], lhsT=qT[r0:r0 + 64, j, s4 * 128:(s4 + 1) * 128],
                                         rhs=kmT[r0:r0 + 64, j, :], start=True, stop=True)
                    return r
                P.op("pe", gate, reads=["qT0", "qT1", "qT2", "qT3", "kmT"], writes=["ps5"])
                P.op("dve", lambda e, t=t: e.tensor_tensor(out=gm.rearrange("p h s n -> p (h s n)"), in0=bank(5)[:, 0:256],
                                                           in1=cm[:, t - 2].rearrange("p h s n -> p (h s n)"), op=ALU.add),
                     reads=["ps5", "cm"], writes=["gm"])

                def mx(e):
                    r = None
                    for h in range(8):
                        for s4 in range(4):
                            r = e.max(out=top[:, h, s4, :], in_=gm[:, h, s4, :])
                    return r
                P.op("dve", mx, reads=["gm"], writes=["top"])

                def sel(e, t=t):
                    r = None
                    for h in range(8):
                        for s4 in range(4):
                            i = 2 * t + s4 // 2
                            idx = 10 - i
                            r = e.tensor_scalar(out=Mm[:, h, s4, :], in0=gm[:, h, s4, :], scalar1=top[:, h, s4, idx:idx + 1],
                                                scalar2=-1.0, op0=ALU.is_ge, op1=ALU.add)
                    return r
                P.op("dve", sel, reads=["gm", "top"], writes=["Mm"])

            nlt = 4 * t + 4
            pfi = {"i": 0}
            pbi = {"i": 0}
            for h in range(8):
                j, r0 = h // 2, 64 * (h % 2)
                mt = h % 2
                if masked_tile:
                    b = next_bank()

                    def tr(e, h=h, b=b):
                        r = None
                        for s4 in range(4):
                            r = e.transpose(out=bank(b)[0:8, s4 * 128:(s4 + 1) * 128], in_=Mm[:, h, s4, :], identity=ident[:])
                        return r
                    P.op("pe", tr, reads=["Mm", "ident"], writes=["ps%d" % b])
                    P.op("act", lambda e, b=b, mt=mt: e.activation(out=MT[mt][0:8, :], in_=bank(b)[0:8, :], func=AF.Copy),
                         reads=["ps%d" % b], writes=["MT%d" % mt])
                ab = 6 + (h % 2)
                abn = "ps%d" % ab
                for lt in range(nlt):
                    cs = max(0, lt * 128 - t0)
                    n0 = lt // 2
                    msk = masked_tile and n0 < 2 * t + 1
                    b = next_bank()
                    ltile = lt // 4

                    def qk(e, b=b, lt=lt, cs=cs, msk=msk, n0=n0, j=j, r0=r0, mt=mt):
                        r = e.matmul(bank(b)[:, cs:512], lhsT=kT[r0:r0 + 64, j, lt * 128:(lt + 1) * 128],
                                     rhs=qT[r0:r0 + 64, j, cs:512], start=True, stop=(not msk))
                        if msk:
                            r = e.matmul(bank(b)[:, cs:512], lhsT=eall[0:8, n0, :], rhs=MT[mt][0:8, cs:512],
                                         start=False, stop=True)
                        return r
                    rd = ["kT.%d.%d" % (j, ltile), "qT%d" % j]
                    if msk:
                        rd += ["MT%d" % mt, "eall"]
                    P.op("pe", qk, reads=rd, writes=["ps%d" % b])
                    delta = t0 - lt * 128
                    far = delta >= 1024
                    pi = pbi["i"] % 3
                    pbi["i"] += 1
                    if far:
                        P.op("act", lambda e, b=b, pi=pi, h=h, cs=cs: e.activation(
                            out=Pb[pi][:, cs:512], in_=bank(b)[:, cs:512], func=AF.Exp, bias=vcol(V_TB + h), scale=SCALE),
                            reads=["ps%d" % b, "vecs"], writes=["Pb%d" % pi])
                    else:
                        fi = pfi["i"] % 2
                        pfi["i"] += 1
                        P.op("act", lambda e, b=b, fi=fi, cs=cs: e.activation(
                            out=Pf[fi][:, cs:512], in_=bank(b)[:, cs:512], func=AF.Exp, scale=SCALE),
                            reads=["ps%d" % b], writes=["Pf%d" % fi])
                        P.op("dve", lambda e, fi=fi, pi=pi, h=h, cs=cs, delta=delta: e.tensor_tensor(
                            out=Pb[pi][:, cs:512], in0=Pf[fi][:, cs:512], in1=EB[:, h, delta + cs:delta + 512], op=ALU.mult),
                            reads=["Pf%d" % fi, "EB"], writes=["Pb%d" % pi])
                    vc0 = 0 if h % 2 == 0 else 64
                    P.op("pe", lambda e, ab=ab, lt=lt, j=j, vc0=vc0, pi=pi, cs=cs, nlt=nlt: e.matmul(
                        bank(ab)[:, cs:512], lhsT=Vaug[:, lt, j, vc0:vc0 + 128], rhs=Pb[pi][:, cs:512],
                        start=(lt == 0), stop=(lt == nlt - 1)),
                        reads=["Pb%d" % pi, "V.%d.%d.a" % (ltile, j), "V.%d.%d.b" % (ltile, j), "Vall"], writes=[abn])
                a0, s0 = (0, 64) if h % 2 == 0 else (64, 0)
                P.op("dve", lambda e, ab=ab, a0=a0, s0=s0: e.reciprocal(out=rs[a0:a0 + 64, :], in_=bank(ab)[s0:s0 + 64, :]),
                     reads=[abn], writes=["rs"])
                P.op("dve", lambda e, ab=ab, a0=a0, j=j: e.tensor_tensor(out=mixT[a0:a0 + 64, j, :], in0=bank(ab)[a0:a0 + 64, :],
                                                                        in1=rs[a0:a0 + 64, :], op=ALU.mult),
                     reads=[abn, "rs"], writes=["mix%d" % j])

            if stop == 5 and t == stop_t:
                return
            proj_ln(L, 0, t0, xn, 8,
                    lambda wi, c: wbufM[wi][:, c, :], lambda c: mixT[:, c, :],
                    lambda dc: wload(wbout[L, dc], "wbout%d" % L),
                    ["wM%d" % i for i in range(4)], slotsM, "m_s", ["mix%d" % c for c in range(8)])
            if stop is not None and stop >= 20 and t == stop - 20:
                return

    urot = {"i": 0}
    drot = {"i": 0}
    rrot = {"i": 0}

    def ffn(L):
        P.add_barrier()
        for hf in range(2):
            h0 = hf * 1024
            xnames_h = ["xres.%d.%d" % (tt, dc) for tt in (2 * hf, 2 * hf + 1) for dc in range(8)]
            if hf == 0:
                P.op("pool", lambda e: e.memset(x1bf[:, :, 0:2], 0.0), writes=["x1bf"])
                P.op("pool", lambda e: e.tensor_copy(out=x1bf[:, :, 2:1026], in_=xres[:, :, 0:1024]),
                     reads=xnames_h + ["x1bf"], writes=["x1bf"])
                P.op("pool", lambda e: e.tensor_copy(out=xhalo, in_=x1bf[:, :, 1024:1026]),
                     reads=["x1bf"], writes=["xhalo"])
            else:
                P.op("pool", lambda e: e.tensor_copy(out=x1bf[:, :, 0:2], in_=xhalo),
                     reads=["xhalo", "x1bf"], writes=["x1bf"])
                P.op("pool", lambda e: e.tensor_copy(out=x1bf[:, :, 2:1026], in_=xres[:, :, 1024:2048]),
                     reads=xnames_h + ["x1bf"], writes=["x1bf"])
            for j in range(NJ):
                cres = {}
                for which in range(2):
                    cc = j + which * NJ
                    wi = urot["i"] % 4
                    urot["i"] += 1
                    P.op("dma", lambda e, wi=wi, cc=cc: e.dma_start(out=wbufU[wi].rearrange("p k n -> p (k n)"), in_=wbup[L, cc]),
                         reads=["wbup%d" % L], writes=["wU%d" % wi], dkey="wU%d" % wi)
                    wof = V_FCW + L * 132 + cc * 3
                    bof = V_FCB + L * 44 + cc
                    for s2 in range(2):
                        rg = rrot["i"] % 4
                        rrot["i"] += 1
                        ba, bb = 2 * rg, 2 * rg + 1
                        reg = ps[:, ba * 512 + 510: bb * 512 + 512]

                        def mmu(e, wi=wi, ba=ba, bb=bb, s2=s2):
                            r = None
                            for kc in range(8):
                                e.matmul(bank(ba)[:, 510:512], lhsT=wbufU[wi][:, kc, :], rhs=x1bf[:, kc, s2 * 512:s2 * 512 + 2],
                                         start=(kc == 0), stop=(kc == 7))
                            for kc in range(8):
                                r = e.matmul(bank(bb), lhsT=wbufU[wi][:, kc, :], rhs=x1bf[:, kc, 2 + s2 * 512:2 + (s2 + 1) * 512],
                                             start=(kc == 0), stop=(kc == 7))
                            return r
                        P.op("pe", mmu, reads=["wU%d" % wi, "x1bf"], writes=["ps%d" % ba, "ps%d" % bb])
                        si = (which * 2 + s2)
                        a_ = slotsF[si][:, 0:512]
                        an = "f_a%d" % si
                        P.op("act", lambda e, a_=a_, reg=reg, wof=wof, bof=bof: e.activation(
                            out=a_, in_=reg[:, 2:514], func=AF.Identity, bias=vcol(bof), scale=vcol(wof + 2)),
                            reads=["ps%d" % ba, "ps%d" % bb, "vecs"], writes=[an])
                        P.op("dve", lambda e, a_=a_, reg=reg, wof=wof: e.scalar_tensor_tensor(
                            out=a_, in0=reg[:, 1:513], scalar=vcol(wof + 1), in1=a_, op0=ALU.mult, op1=ALU.add),
                            reads=["ps%d" % ba, "ps%d" % bb, an], writes=[an])
                        P.op("dve", lambda e, a_=a_, reg=reg, wof=wof: e.scalar_tensor_tensor(
                            out=a_, in0=reg[:, 0:512], scalar=vcol(wof), in1=a_, op0=ALU.mult, op1=ALU.add),
                            reads=["ps%d" % ba, "ps%d" % bb, an], writes=[an])
                        cres[(which, s2)] = (a_, an)
                for s2 in range(2):
                    au, aun = cres[(0, s2)]
                    ag, agn = cres[(1, s2)]
                    sg = slotsF[4 + s2][:, 0:512]
                    sgn = "f_sg%d" % s2
                    P.op("act", lambda e, ag=ag, sg=sg: e.activation(out=sg, in_=ag, func=AF.Silu),
                         reads=[agn], writes=[sgn])
                    P.op("pool", lambda e, au=au, sg=sg, j=j, s2=s2: e.tensor_tensor(
                        out=hid[:, j, s2 * 512:(s2 + 1) * 512], in0=au, in1=sg, op=ALU.mult),
                        reads=[aun, sgn], writes=["hid.%d.%d" % (j, s2)])
            for s2 in range(2):
                t = 2 * hf + s2
                t0 = t * 512
                xn = ["xres.%d.%d" % (t, dc) for dc in range(8)]

                def wl(dc):
                    wi = drot["i"] % 2
                    drot["i"] += 1
                    P.op("dma", lambda e, wi=wi, dc=dc: e.dma_start(out=wbufD[wi].rearrange("p j n -> p (j n)"), in_=wbdn[L, dc]),
                         reads=["wbdn%d" % L], writes=["wD%d" % wi], dkey="wD%d" % wi)
                    return wi
                proj_ln(L, 2, t0, xn, NJ,
                        lambda wi, c: wbufD[wi][:, c, :], lambda c, s2=s2: hid[:, c, s2 * 512:(s2 + 1) * 512],
                        wl, ["wD0", "wD1"], slotsF[6:], "f_s", ["hid.%d.%d" % (jj, s2) for jj in range(NJ)])

    for s in range(nseq):
        P.add_barrier()
        for g in range(4):
            for tt in range(4):
                tok0 = g * 512 + tt * 128
                xi = (g * 4 + tt) % 2
                P.op("dma", lambda e, xi=xi, tok0=tok0, s=s: e.dma_start(out=xin[xi], in_=x_d[s, tok0:tok0 + 128, :]),
                     writes=["xin%d" % xi], dkey="xin%d" % xi)
                def trx(e, xi=xi, tt=tt):
                    r = None
                    for c in range(8):
                        r = e.transpose(out=bank(c)[:, tt * 128:(tt + 1) * 128], in_=xin[xi][:, c * 128:(c + 1) * 128], identity=ident[:])
                    return r
                P.op("pe", trx, reads=["xin%d" % xi, "ident"], writes=["ps%d" % c for c in range(8)])
            for c in range(8):
                eng = "act" if c % 2 == 0 else "dve"
                if eng == "act":
                    P.op("act", lambda e, c=c, g=g: e.activation(out=xres[:, c, g * 512:(g + 1) * 512], in_=bank(c), func=AF.Copy),
                         reads=["ps%d" % c], writes=["xres.%d.%d" % (g, c)])
                else:
                    P.op("dve", lambda e, c=c, g=g: e.tensor_copy(out=xres[:, c, g * 512:(g + 1) * 512], in_=bank(c)),
                         reads=["ps%d" % c], writes=["xres.%d.%d" % (g, c)])
        for L in range(nlayers):
            if do_mixer:
                mixer(L)
            if do_ffn:
                ffn(L)
        P.add_barrier()
        for g in range(4):
            for tt in range(4):
                tok0 = g * 512 + tt * 128
                xi = (g * 4 + tt) % 2
                hb = (tt % 2) * 2

                def tro(e, tok0=tok0, hb=hb):
                    r = None
                    for c in range(8):
                        r = e.transpose(out=ps[:, hb * 512 + c * 128: hb * 512 + (c + 1) * 128], in_=xres[:, c, tok0:tok0 + 128], identity=ident[:])
                    return r
                P.op("pe", tro, reads=["xres.%d.%d" % (g, c) for c in range(8)] + ["ident"], writes=["ps%d" % hb, "ps%d" % (hb + 1)])
                P.op("act" if tt % 2 == 0 else "dve",
                     (lambda e, xi=xi, hb=hb: e.activation(out=xin[xi], in_=ps[:, hb * 512:hb * 512 + 1024], func=AF.Copy)) if tt % 2 == 0
                     else (lambda e, xi=xi, hb=hb: e.tensor_copy(out=xin[xi], in_=ps[:, hb * 512:hb * 512 + 1024])),
                     reads=["ps%d" % hb, "ps%d" % (hb + 1)], writes=["xin%d" % xi])
                P.op("dma", lambda e, xi=xi, tok0=tok0, s=s: e.dma_start(out=out_d[s, tok0:tok0 + 128, :], in_=xin[xi]),
                     reads=["xin%d" % xi], dkey="xin%d" % xi)
    P.add_barrier()

    keys = sorted({o["dkey"] for o in P.ops if o["eng"] == "dma"})
    for k in ["pe", "act", "dve", "pool"] + keys:
        SEM[k] = es.enter_context(nc.semaphore("s_" + k))
    final = P.barrier
    queues = {"pe": [], "act": [], "dve": [], "pool": [], "sp": []}
    for o in P.ops:
        queues[o["queue"]].append(o)
    block = es.enter_context(nc.Block())

    def emit(qname, eng):
        waited = {}
        cur_barrier = None

        def wait(k, v):
            if waited.get(k, 0) < v:
                eng.wait_ge(SEM[k], v)
                waited[k] = v
        for o in queues[qname]:
            if o["barrier"] is not None and o["barrier"] is not cur_barrier:
                cur_barrier = o["barrier"]
                for k, v in cur_barrier.items():
                    wait(k, v)
            need = {}
            for d in o["deps"]:
                k, v = P.ops[d]["done"]
                need[k] = max(need.get(k, 0), v)
            for k, v in need.items():
                wait(k, v)
            ins = o["fn"](eng)
            if o["eng"] == "dma":
                ins.then_inc(SEM[o["dkey"]], 16)
            else:
                ins.then_inc(SEM[o["eng"]], 1)
        if qname == "sp":
            for k, v in final.items():
                wait(k, v)

    @block.sync
    def _(e):
        emit("sp", e)

    @block.tensor
    def _(e):
        emit("pe", e)

    @block.scalar
    def _(e):
        emit("act", e)

    @block.vector
    def _(e):
        emit("dve", e)

    @block.gpsimd
    def _(e):
        emit("pool", e)

    es.close()
    return nc


def _t5_bucket(dist):
    n = jnp.maximum(dist, 0)
    max_exact = 16
    nf = jnp.maximum(n, 1).astype(jnp.float32)
    large = max_exact + (jnp.log(nf / max_exact) / math.log(1024 / max_exact) * (32 - max_exact)).astype(jnp.int32)
    large = jnp.minimum(large, 31)
    return jnp.where(n < max_exact, n, large)


def _host_layout(inp):
    f = lambda a: np.asarray(a, dtype=np.float32)
    vec = np.zeros((128, NV), np.float32)
    pc = lambda a: a.reshape(a.shape[:-1] + (-1, 128))
    ln = np.stack([f(inp["ln1_g"]), f(inp["ln1_b"]), f(inp["ln2_g"]), f(inp["ln2_b"])], axis=1)
    vec[:, V_LN:V_LN + 128] = pc(ln).transpose(3, 0, 1, 2).reshape(128, -1)
    cw = f(inp["conv_w"])
    vec[:, V_CVW:V_CVW + 24] = pc(cw).transpose(3, 0, 2, 1).reshape(128, -1)
    vec[:, V_PSC:V_PSC + 8] = pc(f(inp["pool_scale"])).transpose(2, 0, 1).reshape(128, -1)
    fw = f(inp["ffn_conv_w"])
    vec[:, V_FCW:V_FCW + 528] = pc(fw).transpose(3, 0, 2, 1).reshape(128, -1)
    vec[:, V_FCB:V_FCB + 176] = pc(f(inp["ffn_conv_b"])).transpose(2, 0, 1).reshape(128, -1)
    rb = f(inp["rel_bias"])
    vec[:, V_TB:V_TB + 8] = rb[:, 31][None, :]
    wins = np.array([[2, 4], [8, 16]], np.float32)
    for c in range(2):
        for half in range(2):
            w = wins[c, half]
            p0 = half * 64
            vec[p0:p0 + 64, V_RDV + c * 16:V_RDV + (c + 1) * 16] = 1.0 / np.minimum(np.arange(1, 17, dtype=np.float32), w)[None, :]
            vec[p0:p0 + 64, V_RW + c] = 1.0 / w
    wp = f(inp["w_pool"])
    bd = np.zeros((DEPTH, 128, 2, 128), np.float32)
    for c in range(2):
        for g in range(2):
            bd[:, g * 64:(g + 1) * 64, c, g * 64:(g + 1) * 64] = wp[:, 2 * c + g]
    p = np.arange(128)[:, None]
    jj = np.arange(EBW)[None, :]
    dist = jj - p
    try:
        with jax.default_device(jax.devices("cpu")[0]):
            bk = np.asarray(_t5_bucket(jnp.asarray(dist, dtype=jnp.int32)))
    except Exception:
        nn_ = np.maximum(dist, 0)
        nf_ = np.maximum(nn_, 1).astype(np.float32)
        lg_ = 16 + (np.log(nf_ / np.float32(16)) / np.float32(math.log(64.0)) * np.float32(16)).astype(np.int32)
        bk = np.where(nn_ < 16, nn_, np.minimum(lg_, 31))
    bst = rb[:, bk]
    bst = np.where((dist >= 0)[None], bst, np.float32(-1e30)).astype(np.float32)
    cmv = np.zeros((128, 2, 8, 4, 8), np.float32)
    for t in (2, 3):
        for s4 in range(4):
            i = 2 * t + s4 // 2
            cmv[:, t - 2, :, s4, i:] = 1e30
    ea = np.zeros((8, 8, 128), np.float32)
    for n in range(8):
        ea[n, n, :] = BIG
    return dict(
        vecs=vec, wpool_bd=bd, bstrip=bst, ident=np.eye(128, dtype=np.float32),
        cm=cmv.reshape(128, 512), eall=ea.reshape(8, 1024).astype(ml_dtypes.bfloat16),
        w_in=f(inp["w_in"]), w_out=f(inp["w_out"]), w_up=f(inp["w_up"]), w_down=f(inp["w_down"]),
    )


_NC = None
SINGLE_CORE_LAUNCHES = True


def kernel(**inputs):
    global _NC
    shared = _host_layout(inputs)
    x = np.asarray(inputs["x"], dtype=np.float32)
    if _NC is None:
        nc = bass.Bass("TRN2", target_bir_lowering=False)
        _NC = build_program(nc)
    outs = []
    if SINGLE_CORE_LAUNCHES:
        for c in range(8):
            m = dict(shared)
            m["x"] = np.ascontiguousarray(x[2 * c:2 * c + 2])
            res = run_bass_kernel_spmd(_NC, [m], core_ids=[0])
            outs.append(np.asarray(res.results[0]["out"]))
    else:
        in_maps = []
        for c in range(8):
            m = dict(shared)
            m["x"] = np.ascontiguousarray(x[2 * c:2 * c + 2])
            in_maps.append(m)
        res = run_bass_kernel_spmd(_NC, in_maps, core_ids=list(range(8)))
        outs = [np.asarray(r["out"]) for r in res.results]
    out = np.concatenate(outs, axis=0)
    return out.astype(np.float32)
```
